# Optimizing a Trainium2 kernel written in Bass

```python
import math
import jax, jax.numpy as jnp
from jax import lax
import numpy as np

D_MODEL = 1024
BATCH = 4
SEQ = 4096
DEPTH = 1

ATT_GROUPS = ((128, 1), (512, 4), (2048, 16))
ATT_HEADS_PER_GROUP = 8
ATT_HEAD_DIM = 64
N_ATT_HEADS = len(ATT_GROUPS) * ATT_HEADS_PER_GROUP
ATT_QKV = N_ATT_HEADS * ATT_HEAD_DIM
ATT_WIDTH = ATT_HEADS_PER_GROUP * ATT_HEAD_DIM
ALIBI_MAX_EXP = 8.0
NEG_INF = -1e30

HGRN_EXPAND = 128
HGRN_HEADS = D_MODEL // HGRN_EXPAND
HGRN_KEY = HGRN_EXPAND
HGRN_VAL = D_MODEL // HGRN_HEADS
HGRN_FDIM = HGRN_HEADS * HGRN_KEY
HGRN_WIDTH = HGRN_HEADS * HGRN_VAL
HGRN_CHUNK = 64

D_FF = ((8 * D_MODEL + 3 * 256 - 1) // (3 * 256)) * 256

DEEPNORM_ALPHA = (2.0 * DEPTH) ** 0.25
DEEPNORM_BETA = (8.0 * DEPTH) ** -0.25
LN_EPS = 1e-5
RMS_EPS = 1e-6

IN_SPLITS = (ATT_QKV, ATT_QKV, ATT_QKV,
             HGRN_FDIM, HGRN_FDIM, HGRN_FDIM,
             HGRN_WIDTH, HGRN_WIDTH,
             D_MODEL, D_MODEL)
IN_COL_SCALE = (1.0, 1.0, DEEPNORM_BETA, 1.0, 1.0, 1.0, DEEPNORM_BETA, 1.0, 1.0, 1.0)
IN_COLS = sum(IN_SPLITS)

kernel_name = "hybrid_dilated_attn_hgrn2_deepnorm_block"


def layer_norm(x, g, b):
    xf = x.astype(jnp.float32)
    mu = jnp.mean(xf, axis=-1, keepdims=True)
    xc = xf - mu
    var = jnp.mean(xc * xc, axis=-1, keepdims=True)
    y = xc * lax.rsqrt(var + LN_EPS) * g.astype(jnp.float32) + b.astype(jnp.float32)
    return y.astype(x.dtype)


def alibi_slopes(n):
    return 2.0 ** (-ALIBI_MAX_EXP * jnp.arange(1, n + 1, dtype=jnp.float32) / n)


def banded_dilated_softmax(q, k, v, half, dil, slopes):
    n, m, h, dh = q.shape
    nb = -(-m // half)
    mp = nb * half
    qb = jnp.pad(q.astype(jnp.float32), ((0, 0), (0, mp - m), (0, 0), (0, 0))).reshape(n, nb, half, h, dh)
    pad_kv = ((0, 0), (half, mp - m + half), (0, 0), (0, 0))

    def band(t):
        tb = jnp.pad(t.astype(jnp.float32), pad_kv).reshape(n, nb + 2, half, h, dh)
        return jnp.concatenate([tb[:, :-2], tb[:, 1:-1], tb[:, 2:]], axis=2)

    kb, vb = band(k), band(v)
    s = jnp.einsum('nbqhd,nbkhd->nbhqk', qb, kb) * (dh ** -0.5)
    q_pos = jnp.arange(half)
    k_off = jnp.arange(3 * half) - half
    rel = k_off[None, :] - q_pos[:, None]
    k_idx = jnp.arange(nb)[:, None] * half + k_off[None, :]
    valid = (jnp.abs(rel) <= half)[None] & ((k_idx >= 0) & (k_idx < m))[:, None, :]
    dist = (dil * jnp.abs(rel)).astype(jnp.float32)
    bias = -slopes.astype(jnp.float32)[:, None, None] * dist[None]
    s = jnp.where(valid[None, :, None], s + bias[None, None], NEG_INF)
    mx = jnp.max(s, axis=-1, keepdims=True)
    p = jnp.exp(s - mx)
    l = jnp.sum(p, axis=-1, keepdims=True)
    o = jnp.einsum('nbhqk,nbkhd->nbqhd', p, vb) / jnp.transpose(l, (0, 1, 3, 2, 4))
    lse = jnp.transpose((mx + jnp.log(l))[..., 0], (0, 1, 3, 2))
    return o.reshape(n, mp, h, dh)[:, :m], lse.reshape(n, mp, h)[:, :m]


def dilated_group(q, k, v, window, dil, slopes):
    b, s, h, dh = q.shape
    m = s // dil
    half = window // (2 * dil)

    def to_res(t):
        return t.reshape(b, m, dil, h, dh).transpose(0, 2, 1, 3, 4).reshape(b * dil, m, h, dh)

    o, lse = banded_dilated_softmax(to_res(q), to_res(k), to_res(v), half, dil, slopes)
    o = o.reshape(b, dil, m, h, dh).transpose(0, 2, 1, 3, 4).reshape(b, s, h, dh)
    lse = lse.reshape(b, dil, m, h).transpose(0, 2, 1, 3).reshape(b, s, h)
    return o, lse


def dilated_attention(aq, ak, av, slopes):
    b, s, _ = aq.shape
    shp = (b, s, len(ATT_GROUPS), ATT_HEADS_PER_GROUP, ATT_HEAD_DIM)
    aq, ak, av = aq.reshape(shp), ak.reshape(shp), av.reshape(shp)
    outs, lses = [], []
    for g, (window, dil) in enumerate(ATT_GROUPS):
        o, lse = dilated_group(aq[:, :, g], ak[:, :, g], av[:, :, g], window, dil, slopes[g])
        outs.append(o)
        lses.append(lse)
    w = jax.nn.softmax(jnp.stack(lses), axis=0)
    o = jnp.sum(w[..., None] * jnp.stack(outs), axis=0)
    return o.reshape(b, s, ATT_WIDTH).astype(aq.dtype)


def gla_chunkwise(q, k, v, logf):
    b, h, s, kd = q.shape
    vd = v.shape[-1]
    c = HGRN_CHUNK
    n = s // c
    q = q.reshape(b, h, n, c, kd)
    k = k.reshape(b, h, n, c, kd)
    v = v.reshape(b, h, n, c, vd)
    cum = jnp.cumsum(logf.reshape(b, h, n, c, kd), axis=3)
    q_dec = q * jnp.exp(cum)
    k_inv = k * jnp.exp(-cum)
    a = jnp.einsum('bhnik,bhnjk->bhnij', q_dec, k_inv)
    causal_in_chunk = jnp.tril(jnp.ones((c, c), dtype=bool))
    a = jnp.where(causal_in_chunk, a, 0.0)
    o_intra = jnp.einsum('bhnij,bhnjv->bhniv', a, v)
    cum_last = cum[..., -1:, :]
    u = jnp.einsum('bhnck,bhncv->bhnkv', k * jnp.exp(cum_last - cum), v)
    decay = jnp.exp(cum_last[..., 0, :])

    def step(state, inp):
        d, uu = inp
        return d[..., None] * state + uu, state

    init = jnp.zeros((b, h, kd, vd), jnp.float32)
    _, s_start = lax.scan(step, init, (jnp.moveaxis(decay, 2, 0), jnp.moveaxis(u, 2, 0)))
    s_start = jnp.moveaxis(s_start, 0, 2)
    o_inter = jnp.einsum('bhnck,bhnkv->bhncv', q_dec, s_start)
    return (o_intra + o_inter).reshape(b, h, s, vd)


def hgrn2(hq, hf_fwd, hf_bwd, hi, hg, lower_bound, norm_g):
    b, s, _ = hq.shape
    H, K, V = HGRN_HEADS, HGRN_KEY, HGRN_VAL

    def heads(t, d):
        return t.astype(jnp.float32).reshape(b, s, H, d).transpose(0, 2, 1, 3)

    q = heads(jax.nn.silu(hq.astype(jnp.float32)), K) * (K ** -0.5)
    v = heads(hi, V)

    def gates(fpre, lb):
        lb = lb.reshape(H, 1, K)
        f = lb + (1.0 - lb) * jax.nn.sigmoid(heads(fpre, K))
        return 1.0 - f, jnp.log(f)

    k_f, lf_f = gates(hf_fwd, lower_bound[0])
    k_b, lf_b = gates(hf_bwd, lower_bound[1])
    flip = lambda t: jnp.flip(t, axis=2)
    o_fwd = gla_chunkwise(q, k_f, v, lf_f)
    o_bwd = flip(gla_chunkwise(flip(q), flip(k_b), flip(v), flip(lf_b)))
    o = (o_fwd + o_bwd).transpose(0, 2, 1, 3)
    o = o * lax.rsqrt(jnp.mean(o * o, axis=-1, keepdims=True) + RMS_EPS) * norm_g.astype(jnp.float32)
    o = o * jax.nn.silu(hg.astype(jnp.float32)).reshape(b, s, H, V)
    return o.reshape(b, s, HGRN_WIDTH).astype(hq.dtype)


def setup_inputs(seed: int = 0) -> dict:
    key = jax.random.key(seed)
    ks = jax.random.split(key, 20)
    f32 = jnp.float32

    def nrm(k, shape, scale):
        return jax.random.normal(k, shape, f32) * scale

    x = nrm(ks[0], (BATCH, SEQ, D_MODEL), 1.0)
    ln_in_g = 1.0 + nrm(ks[1], (D_MODEL,), 0.02)
    ln_in_b = nrm(ks[2], (D_MODEL,), 0.02)
    piece_keys = jax.random.split(ks[3], len(IN_SPLITS))
    w_in = jnp.concatenate(
        [nrm(pk, (DEPTH, D_MODEL, c), (D_MODEL ** -0.5) * sc)
         for pk, c, sc in zip(piece_keys, IN_SPLITS, IN_COL_SCALE)], axis=-1)
    hgrn_lb = nrm(ks[4], (2, DEPTH + 1, HGRN_FDIM), 0.1)
    hgrn_norm_g = 1.0 + nrm(ks[5], (DEPTH, HGRN_VAL), 0.02)
    w_att_up = nrm(ks[6], (DEPTH, ATT_WIDTH, D_MODEL), ATT_WIDTH ** -0.5)
    w_hgrn_up = nrm(ks[7], (DEPTH, HGRN_WIDTH, D_MODEL), HGRN_WIDTH ** -0.5)
    w_o = nrm(ks[8], (DEPTH, D_MODEL, D_MODEL), (D_MODEL ** -0.5) * DEEPNORM_BETA)
    ln1_g = 1.0 + nrm(ks[9], (DEPTH, D_MODEL), 0.02)
    ln1_b = nrm(ks[10], (DEPTH, D_MODEL), 0.02)
    w_ffn_in = nrm(ks[11], (DEPTH, D_MODEL, 2 * D_FF), (D_MODEL ** -0.5) * DEEPNORM_BETA)
    w_ffn_out = nrm(ks[12], (DEPTH, D_FF, D_MODEL), (D_FF ** -0.5) * DEEPNORM_BETA)
    ln2_g = 1.0 + nrm(ks[13], (DEPTH, D_MODEL), 0.02)
    ln2_b = nrm(ks[14], (DEPTH, D_MODEL), 0.02)
    return {"x": x, "ln_in_g": ln_in_g, "ln_in_b": ln_in_b, "w_in": w_in,
            "hgrn_lb": hgrn_lb, "hgrn_norm_g": hgrn_norm_g,
            "w_att_up": w_att_up, "w_hgrn_up": w_hgrn_up, "w_o": w_o,
            "ln1_g": ln1_g, "ln1_b": ln1_b, "w_ffn_in": w_ffn_in, "w_ffn_out": w_ffn_out,
            "ln2_g": ln2_g, "ln2_b": ln2_b}


def reference(x, ln_in_g, ln_in_b, w_in, hgrn_lb, hgrn_norm_g, w_att_up, w_hgrn_up, w_o,
              ln1_g, ln1_b, w_ffn_in, w_ffn_out, ln2_g, ln2_b):
    slopes = alibi_slopes(N_ATT_HEADS).reshape(len(ATT_GROUPS), ATT_HEADS_PER_GROUP)
    lower_bounds = jnp.cumsum(jax.nn.softmax(hgrn_lb.astype(jnp.float32), axis=1), axis=1)
    split_idx = [int(i) for i in np.cumsum(IN_SPLITS)[:-1]]
    h = layer_norm(x, ln_in_g, ln_in_b)
    for l in range(DEPTH):
        proj = h @ w_in[l]
        aq, ak, av, hq, hf_fwd, hf_bwd, hi, hg, ga, gh = jnp.split(proj, split_idx, axis=-1)
        att = dilated_attention(aq, ak, av, slopes)
        rec = hgrn2(hq, hf_fwd, hf_bwd, hi, hg, lower_bounds[:, l], hgrn_norm_g[l])
        merged = jax.nn.sigmoid(ga) * (att @ w_att_up[l]) + jax.nn.sigmoid(gh) * (rec @ w_hgrn_up[l])
        h = layer_norm(DEEPNORM_ALPHA * h + merged @ w_o[l], ln1_g[l], ln1_b[l])
        gate, up = jnp.split(h @ w_ffn_in[l], 2, axis=-1)
        h = layer_norm(DEEPNORM_ALPHA * h + (jax.nn.silu(gate) * up) @ w_ffn_out[l], ln2_g[l], ln2_b[l])
    return h
```

```python
import contextlib
import numpy as np
import concourse.bass as bass
import concourse.mybir as mybir
from concourse.bass_utils import run_bass_kernel_spmd

F32 = mybir.dt.float32
BF16 = mybir.dt.bfloat16
AF = mybir.ActivationFunctionType
ALU = mybir.AluOpType

S = 4096
D = 1024
NOWN = 2048
NHT = 3072
DFF = 2816
NJ = 22
ALPHA = 2.0 ** 0.25
LN_EPS = 1e-5
RMS_EPS = 1e-6
GROUPS = ((128, 1), (512, 4), (2048, 16))
N_DMA_SEMS = 24
SAME_ENGINE_SYNC = True


class Reg:
    __slots__ = ("w", "r", "excl")

    def __init__(self, excl=False):
        self.w = {}
        self.r = {}
        self.excl = excl


class Ctx:
    def __init__(self, nc, es):
        self.nc = nc
        self.eng = {"pe": nc.tensor, "act": nc.scalar, "dve": nc.vector,
                    "pool": nc.gpsimd, "sp": nc.sync}
        self.sem, self.cnt = {}, {}
        self.known = {e: {} for e in self.eng}
        for e in self.eng:
            self.sem[e] = es.enter_context(nc.semaphore("s_" + e))
            self.cnt[e] = 0
        self.dma_sems = {"sp": [], "pool": [], "act": []}
        for q in ("sp", "pool"):
            for i in range(N_DMA_SEMS // 2):
                k = "dma_%s%d" % (q, i)
                self.sem[k] = es.enter_context(nc.semaphore("s_" + k))
                self.cnt[k] = 0
                self.dma_sems[q].append(k)
        self.dma_rr = {"sp": 0, "pool": 0}

    def _wait(self, e, deps):
        eo, kn = self.eng[e], self.known[e]
        for k, v in deps.items():
            if k == e and (e == "pe" or not SAME_ENGINE_SYNC):
                continue
            if kn.get(k, 0) < v:
                eo.wait_ge(self.sem[k], v)
                kn[k] = v

    @staticmethod
    def _deps(reads, writes):
        deps = {}
        for r in reads:
            for k, v in r.w.items():
                if deps.get(k, 0) < v:
                    deps[k] = v
            if r.excl:
                for k, v in r.r.items():
                    if deps.get(k, 0) < v:
                        deps[k] = v
        for w in writes:
            for k, v in w.w.items():
                if deps.get(k, 0) < v:
                    deps[k] = v
            for k, v in w.r.items():
                if deps.get(k, 0) < v:
                    deps[k] = v
        return deps

    def op(self, e, fn, reads=(), writes=()):
        self._wait(e, self._deps(reads, writes))
        ins = fn(self.eng[e])
        ins.then_inc(self.sem[e], 1)
        self.cnt[e] += 1
        n = self.cnt[e]
        for r in reads:
            r.r[e] = n
        for w in writes:
            w.w[e] = n
            w.r = {}
        return ins

    def dma(self, q, out, in_, reads=(), writes=()):
        deps = self._deps(reads, writes)
        k = self.dma_sems[q][self.dma_rr[q]]
        self.dma_rr[q] = (self.dma_rr[q] + 1) % len(self.dma_sems[q])
        if self.cnt[k] > 0:
            deps[k] = max(deps.get(k, 0), self.cnt[k])
        self._wait(q, deps)
        ins = self.eng[q].dma_start(out=out, in_=in_)
        ins.then_inc(self.sem[k], 16)
        self.cnt[k] += 16
        v = self.cnt[k]
        for r in reads:
            r.r[k] = v
        for w in writes:
            w.w[k] = v
            w.r = {}
        return ins

    def barrier(self):
        deps = {k: v for k, v in self.cnt.items() if v > 0}
        for e in self.eng:
            d = dict(deps)
            d.pop(e, None)
            self._wait(e, d)

    def finish(self, e="sp"):
        deps = {k: v for k, v in self.cnt.items() if v > 0 and k != e}
        self._wait(e, deps)


class Arena:
    def __init__(self, t, nbytes):
        self.t, self.n, self.top = t, nbytes, 0

    def mark(self):
        return self.top

    def release(self, m):
        self.top = m

    def alloc(self, shape, dt):
        ne = int(np.prod(shape))
        nb = ne * (2 if dt == BF16 else 4)
        off = (self.top + 63) // 64 * 64
        self.top = off + nb
        assert self.top <= self.n, "arena overflow %d > %d" % (self.top, self.n)
        v = self.t[:, off // 2: off // 2 + nb // 2]
        if dt != BF16:
            v = v.bitcast(dt)
        if len(shape) == 2:
            v = v.rearrange("p (a b) -> p a b", a=shape[0])
        elif len(shape) == 3:
            v = v.rearrange("p (a b c) -> p a b c", a=shape[0], b=shape[1])
        elif len(shape) == 4:
            v = v.rearrange("p (a b c d) -> p a b c d", a=shape[0], b=shape[1], c=shape[2])
        return v


def bview(psum_f32_ap):
    return psum_f32_ap.bitcast(BF16)


def build_nc(dbg=None):
    nc = bass.Bass("TRN2", target_bir_lowering=False)

    def din(name, shape):
        return nc.dram_tensor(name, list(shape), F32, kind="ExternalInput").ap()

    x_d = din("x", [S, D])
    cst_d = din("cst", [128, 640])
    pp_d = din("pp", [128, 48])
    rowp_d = din("rowp", [7, D])
    dtab_d = din("dtab", [12, 128, 512])
    w_att_d = din("w_att", [12, 128, 8 * 384])
    w_hg_d = din("w_hg", [4, 128, 8 * 1280])
    w_gate_d = din("w_gate", [8, 128, 8 * 256])
    w_pre_d = din("w_pre", [128, 8 * 2048])
    w_a_d = din("w_a", [128, 4 * 1024])
    w_r_d = din("w_r", [128, 8 * 1024])
    w_o_d = din("w_o", [128, 8 * 1024])
    w_f1_d = din("w_f1", [NJ, 128, 8 * 256])
    w_f2_d = din("w_f2", [8, 128, NJ * 128])
    y_d = nc.dram_tensor("y", [NOWN, D], F32, kind="ExternalOutput").ap()
    dbg_outs = {}

    with contextlib.ExitStack() as es:
        c = Ctx(nc, es)
        ARENA_BYTES = 206 * 1024
        arena_t = es.enter_context(nc.sbuf_tensor("arena", [128, ARENA_BYTES // 2], BF16))
        A = Arena(arena_t, ARENA_BYTES)
        pb = [es.enter_context(nc.psum_tensor("pb%d" % i, [128, 512], F32)) for i in range(8)]
        Rpb = [Reg(excl=True) for _ in range(8)]

        def dump(name, ap, reg, dt=F32):
            shp = list(ap.shape)
            t = nc.dram_tensor(name, shp, dt, kind="ExternalOutput").ap()
            dbg_outs[name] = True
            c.dma("sp", t, ap, reads=[reg])

        ident = A.alloc((128,), BF16)
        identf = A.alloc((128,), F32)
        maskF = A.alloc((128,), F32)
        maskB = A.alloc((128,), F32)
        onesA = A.alloc((128,), BF16)
        onesB = A.alloc((128,), BF16)
        pp = A.alloc((48,), F32)
        lbv = A.alloc((2, 8), F32)
        omlv = A.alloc((2, 8), F32)
        mstat = A.alloc((16, 2), F32)
        S_in = A.alloc((8, 128), F32)
        zeros = A.alloc((128,), F32)
        Rz = Reg()
        Rc = Reg()
        c.dma("pool", ident, cst_d[:, 0:128], writes=[Rc])
        c.dma("sp", identf, cst_d[:, 0:128], writes=[Rc])
        c.dma("sp", maskF, cst_d[:, 128:256], writes=[Rc])
        c.dma("sp", maskB, cst_d[:, 256:384], writes=[Rc])
        c.dma("pool", onesA, cst_d[:, 384:512], writes=[Rc])
        c.dma("pool", onesB, cst_d[:, 512:640], writes=[Rc])
        c.dma("sp", pp, pp_d[:, :], writes=[Rc])
        Rlb = Reg()
        for d in range(2):
            c.op("dve", lambda e: e.tensor_tensor(out=lbv[:, d, :], in0=pp[:, 16 + 16 * d: 24 + 16 * d],
                                                  in1=pp[:, 24 + 16 * d: 32 + 16 * d], op=ALU.subtract), [Rc], [Rlb])
        c.op("act", lambda e: e.activation(out=lbv[:, :, :], in_=lbv[:, :, :], func=AF.Sigmoid), [Rlb], [Rlb])
        c.op("dve", lambda e: e.tensor_scalar(out=omlv[:, :, :], in0=lbv[:, :, :], scalar1=-1.0, scalar2=1.0,
                                              op0=ALU.mult, op1=ALU.add), [Rlb], [Rlb])
        RS = Reg()
        c.op("pool", lambda e: e.memset(S_in[:, :, :], 0.0), [], [RS])
        c.op("pool", lambda e: e.memset(zeros, 0.0), [], [Rz])
        Rms = Reg()
        base_mark = A.mark()

        hT = A.alloc((8, NHT), BF16)
        attT = A.alloc((4, NOWN), BF16)
        recT = A.alloc((8, NOWN), BF16)
        RhT = [Reg() for _ in range(6)]
        RattT = Reg()
        RrecT = Reg()
        markA = A.mark()

        Wpre = A.alloc((8, 2048), BF16)
        RWpre = Reg()
        for k in range(8):
            c.dma("pool", Wpre[:, k, :], w_pre_d[:, k * 2048:(k + 1) * 2048], writes=[RWpre])
        hTs = A.alloc((8, 512), BF16)
        RhTs = Reg()
        NXS = 4
        xt = [A.alloc((D,), F32) for _ in range(NXS)]
        Rxt = [Reg() for _ in range(NXS)]
        xn = [A.alloc((D,), BF16) for _ in range(2)]
        Rxn = [Reg() for _ in range(2)]
        bst = A.alloc((4, 2, 6), F32)
        mv = A.alloc((4, 2), F32)
        rstd4 = A.alloc((4,), F32)
        nmr4 = A.alloc((4,), F32)
        epsc = A.alloc((1,), F32)
        Rst = Reg()
        c.op("pool", lambda e: e.memset(epsc, LN_EPS), [], [Rst])
        vO = A.alloc((4, 1024), BF16)
        RvO = Reg()
        NT = 2
        tmp = [[A.alloc((512,), F32) for _ in range(6)] for _ in range(NT)]
        Rtmp = [Reg() for _ in range(NT)]
        kUT = [A.alloc((512,), BF16) for _ in range(2)]
        RkUT = [Reg() for _ in range(2)]
        kUt = [A.alloc((4, 128), BF16) for _ in range(2)]
        RkUt = [Reg() for _ in range(2)]
        dlO = A.alloc((8, 4), F32)
        RdlO = Reg()

        xtile_ctr = [0]
        evac_rr = [0]

        def evac_copy(out, in_, reads, writes):
            evac_rr[0] ^= 1
            if evac_rr[0]:
                c.op("act", lambda e: e.activation(out=out, in_=in_, func=AF.Copy), reads, writes)
            else:
                c.op("dve", lambda e: e.tensor_copy(out=out, in_=in_), reads, writes)

        def ln_block(blk, dst, Rdst, own):
            slots = []
            for j in range(4):
                s = xtile_ctr[0] % NXS
                xtile_ctr[0] += 1
                slots.append(s)
                t0 = blk * 512 + j * 128
                c.dma("sp", xt[s], x_d[t0:t0 + 128, :], writes=[Rxt[s]])
                for hh in range(2):
                    c.op("dve", lambda e: e.bn_stats(out=bst[:, j, hh, :], in_=xt[s][:, hh * 512:(hh + 1) * 512]),
                         [Rxt[s]], [Rst])
                c.op("dve", lambda e: e.bn_aggr(out=mv[:, j, :], in_=bst[:, j, :, :].rearrange("p a b -> p (a b)")), [Rst], [Rst])
            c.op("act", lambda e: e.activation(out=rstd4, in_=mv[:, :, 1], func=AF.Sqrt, bias=epsc, scale=1.0),
                 [Rst], [Rst])
            c.op("dve", lambda e: e.reciprocal(out=rstd4, in_=rstd4), [Rst], [Rst])
            c.op("dve", lambda e: e.scalar_tensor_tensor(out=nmr4, in0=mv[:, :, 0], scalar=-1.0, in1=rstd4,
                                                         op0=ALU.mult, op1=ALU.mult), [Rst], [Rst])
            if own:
                for j in range(4):
                    tg = blk * 4 + j
                    c.op("pool", lambda e: e.tensor_copy(out=mstat[:, tg, 0:1], in_=rstd4[:, j:j + 1]), [Rst], [Rms])
                    c.op("pool", lambda e: e.tensor_copy(out=mstat[:, tg, 1:2], in_=nmr4[:, j:j + 1]), [Rst], [Rms])
            for j in range(4):
                s = slots[j]
                xs = j % 2
                c.op("act", lambda e: e.activation(out=xn[xs], in_=xt[s], func=AF.Identity,
                                                   bias=nmr4[:, j:j + 1], scale=rstd4[:, j:j + 1]),
                     [Rxt[s], Rst], [Rxn[xs]])
                for k in range(8):
                    bank = k // 2
                    o = bview(pb[bank][:, :])[:, (k % 2) * 512 + j * 128:(k % 2) * 512 + (j + 1) * 128]
                    c.op("pe", lambda e: e.transpose(o, xn[xs][:, k * 128:(k + 1) * 128], ident),
                         [Rxn[xs], Rc], [Rpb[bank]])
            for k in range(8):
                bank = k // 2
                src = bview(pb[bank][:, :])[:, (k % 2) * 512:(k % 2) * 512 + 512]
                c.op("act", lambda e: e.activation(out=dst[:, k, :], in_=src, func=AF.Identity,
                                                   bias=pp[:, 8 + k:9 + k], scale=pp[:, k:k + 1]),
                     [Rpb[bank], Rc], [Rdst])

        def pre_block(hb, Rhb):
            import os as _os
            if _os.environ.get('SKIP_PRE'):
                return
            for j in range(4):
                for hh in range(2):
                    bank = 4 + hh
                    for k in range(8):
                        c.op("pe", lambda e: e.matmul(pb[bank][:, :], lhsT=hb[:, k, j * 128:(j + 1) * 128],
                                                      rhs=Wpre[:, k, 1024 + hh * 512:1024 + (hh + 1) * 512],
                                                      start=(k == 0), stop=(k == 7)), [Rhb, RWpre], [Rpb[bank]])
                    evac_copy(vO[:, j, hh * 512:(hh + 1) * 512], pb[bank][:, :], [Rpb[bank]], [RvO])
            for h in range(8):
                ts = h % NT
                s1, f1, k1, b1, r1, t1 = tmp[ts]
                Rt = Rtmp[ts]
                bank = 6 + (h % 2)
                for k in range(8):
                    c.op("pe", lambda e: e.matmul(pb[bank][:, :], lhsT=Wpre[:, k, h * 128:(h + 1) * 128],
                                                  rhs=hb[:, k, :], start=(k == 0), stop=(k == 7)),
                         [Rhb, RWpre], [Rpb[bank]])
                c.op("act", lambda e: e.activation(out=s1, in_=pb[bank][:, :], func=AF.Sigmoid), [Rpb[bank]], [Rt])
                c.op("dve", lambda e: e.tensor_scalar(out=f1, in0=s1, scalar1=omlv[:, 1, h:h + 1],
                                                      scalar2=lbv[:, 1, h:h + 1], op0=ALU.mult, op1=ALU.add),
                     [Rt, Rlb], [Rt])
                c.op("pool", lambda e: e.tensor_scalar(out=k1, in0=f1, scalar1=-1.0, scalar2=1.0,
                                                       op0=ALU.mult, op1=ALU.add), [Rt], [Rt])
                for j in range(4):
                    sl = slice(j * 128, (j + 1) * 128)
                    c.op("dve", lambda e: e.tensor_tensor_scan(out=b1[:, sl][:, ::-1], data0=f1[:, sl][:, ::-1],
                                                               data1=zeros[:, 0:128], initial=1.0,
                                                               op0=ALU.mult, op1=ALU.add), [Rt, Rz], [Rt])
                c.op("dve", lambda e: e.reciprocal(out=r1, in_=b1), [Rt], [Rt])
                c.op("pool", lambda e: e.tensor_tensor(out=t1, in0=k1, in1=r1, op=ALU.mult), [Rt], [Rt])
                us = h % 2
                b1v = b1.rearrange("p (a b) -> p a b", a=4)
                c.op("pool", lambda e: e.tensor_tensor(out=kUT[us].rearrange("p (a b) -> p a b", a=4),
                                                       in0=t1.rearrange("p (a b) -> p a b", a=4),
                                                       in1=b1v[:, :, 0:1].to_broadcast([128, 4, 128]), op=ALU.mult),
                     [Rt], [RkUT[us]])
                c.op("dve", lambda e: e.tensor_copy(out=dlO[:, h, :], in_=b1v[:, :, 0]), [Rt], [RdlO])
                tb = 2 + (h % 2)
                for j in range(4):
                    o = bview(pb[tb][:, :])[:, j * 128:(j + 1) * 128]
                    c.op("pe", lambda e: e.transpose(o, kUT[us][:, j * 128:(j + 1) * 128], ident),
                         [RkUT[us], Rc], [Rpb[tb]])
                c.op("act", lambda e: e.activation(out=kUt[us].rearrange("p a b -> p (a b)"),
                                                   in_=bview(pb[tb][:, :])[:, 0:512], func=AF.Copy),
                     [Rpb[tb]], [RkUt[us]])
                ub = 0 + (h % 2)
                for j in (3, 2, 1, 0):
                    c.op("pe", lambda e: e.matmul(pb[ub][:, j * 128:(j + 1) * 128], lhsT=kUt[us][:, j, :],
                                                  rhs=vO[:, j, h * 128:(h + 1) * 128], start=True, stop=True),
                         [RkUt[us], RvO], [Rpb[ub]])
                for j in (3, 2, 1, 0):
                    c.op("dve", lambda e: e.scalar_tensor_tensor(out=S_in[:, h, :], in0=S_in[:, h, :],
                                                                 scalar=dlO[:, h, j:j + 1],
                                                                 in1=pb[ub][:, j * 128:(j + 1) * 128],
                                                                 op0=ALU.mult, op1=ALU.add),
                         [RS, RdlO, Rpb[ub]], [RS])

        import os as _os
        _only_att = bool(_os.environ.get('ONLY_ATT'))
        for blk in (() if _only_att else (7, 6, 5, 4)):
            if blk >= 6:
                ln_block(blk, hTs, RhTs, False)
                pre_block(hTs, RhTs)
            else:
                dst = hT[:, :, blk * 512:(blk + 1) * 512]
                ln_block(blk, dst, RhT[blk], False)
                pre_block(dst, RhT[blk])
        for blk in (() if _only_att else range(4)):
            ln_block(blk, hT[:, :, blk * 512:(blk + 1) * 512], RhT[blk], True)

        if dbg == "pre":
            c.barrier()
            dump("dbg_hT", hT, RhT[0], BF16)
            dump("dbg_S", S_in, RS)
            dump("dbg_mstat", mstat, Rms)
            c.barrier()
            c.finish("sp")
            return nc, dbg_outs

        c.barrier()
        A.release(markA)

        Watt = [A.alloc((8, 384), BF16) for _ in range(2)]
        RWatt = [Reg() for _ in range(2)]
        Qz = [A.alloc((NOWN,), BF16) for _ in range(2)]
        KT = A.alloc((NHT,), BF16)
        VT = A.alloc((NHT,), BF16)
        RQT, RKT, RVT = Reg(), Reg(), Reg()
        NVS = 8
        Vt = A.alloc((NVS, 2, 128), BF16)
        RVt = [Reg() for _ in range(NVS)]
        Ee = [A.alloc((2, 256), F32) for _ in range(2)]
        REe = [Reg() for _ in range(2)]
        Pp = [A.alloc((2, 256), BF16) for _ in range(2)]
        RPp = [Reg() for _ in range(2)]
        Dt = [A.alloc((2, 256), F32) for _ in range(2)]
        RDt = [Reg() for _ in range(2)]
        Uacc = A.alloc((NOWN,), F32)
        Lacc = A.alloc((NOWN,), F32)
        Racc = Reg()
        c.op("pool", lambda e: e.memset(Vt.rearrange("p a b c -> p (a b c)"), 0.0), [], RVt)
        for hd_ in range(2):
            c.op("pool", lambda e: e.memset(Qz[hd_], 0.0), [], [RQT])

        pst = bass.AP
        att_iter = [(hp, g) for hp in range(4) for g in range(3)]
        import os as _os
        if _os.environ.get('ATT_LIMIT'):
            att_iter = att_iter[:int(_os.environ['ATT_LIMIT'])]

        def load_watt(i):
            hp, g = att_iter[i]
            s = i % 2
            c.dma("pool", Watt[s].rearrange("p a b -> p (a b)"), w_att_d[hp * 3 + g, :, :], writes=[RWatt[s]])
            c.dma("sp", Dt[s].rearrange("p a b -> p (a b)"), dtab_d[g * 4 + hp, :, :], writes=[RDt[s]])

        load_watt(0)
        proj_rr = [0]
        vslot_ctr = [0]
        ul_ctr = [0]

        for it, (hp, g) in enumerate(att_iter):
            ws = it % 2
            W = Watt[ws]
            if it + 1 < len(att_iter):
                load_watt(it + 1)
            win, dil = GROUPS[g]
            n_own = NOWN // dil
            m_all = n_own + 64
            nq = n_own // 128
            Qzv = [q_.rearrange("p (r i) -> p r i", r=dil) for q_ in Qz]
            KTv = KT[:, 0:dil * m_all].rearrange("p (r i) -> p r i", r=dil)
            VTv = VT[:, 0:dil * m_all].rearrange("p (r i) -> p r i", r=dil)
            ntok_kv = NOWN + 64 * dil
            blocks = []
            t0 = 0
            while t0 < ntok_kv:
                n = min(512, ntok_kv - t0)
                blocks.append((t0, n))
                t0 += n
            for (t0, n) in blocks:
                for which in range(3):
                    if which == 0 and t0 >= NOWN:
                        continue
                    bank = proj_rr[0] % 2
                    proj_rr[0] += 1
                    hreg = RhT[t0 // 512]
                    for k in range(8):
                        c.op("pe", lambda e: e.matmul(pb[bank][:, 0:n], lhsT=W[:, k, which * 128:(which + 1) * 128],
                                                      rhs=hT[:, k, t0:t0 + n], start=(k == 0), stop=(k == 7)),
                             [hreg, RWatt[ws]], [Rpb[bank]])
                    i0, cnt = t0 // dil, n // dil
                    srcv = pb[bank][:, 0:n].rearrange("p (i r) -> p i r", r=dil)
                    rg = (RQT, RKT, RVT)[which]
                    if which == 0:
                        for hd_ in range(2):
                            pr_ = slice(hd_ * 64, (hd_ + 1) * 64)
                            dq = Qzv[hd_][pr_, :, i0:i0 + cnt].rearrange("p r i -> p i r")
                            c.op("act", lambda e: e.activation(out=dq, in_=srcv[pr_], func=AF.Identity, scale=0.125),
                                 [Rpb[bank]], [rg])
                        continue
                    dstv = (None, KTv, VTv)[which][:, :, i0:i0 + cnt].rearrange("p r i -> p i r")
                    if which == 1:
                        c.op("dve", lambda e: e.tensor_copy(out=dstv, in_=srcv), [Rpb[bank]], [rg])
                    else:
                        evac_copy(dstv, srcv, [Rpb[bank]], [rg])
            tiles = [(r, cc) for r in range(dil) for cc in range(nq + 1)]

            def qrange(cc):
                qlo = max(0, 64 - 128 * cc)
                qhi = min(256, n_own + 64 - 128 * cc)
                return qlo, qhi

            def emit_S(ti):
                r, cc = tiles[ti]
                qlo, qhi = qrange(cc)
                nk = 64 if cc == nq else 128
                sb = 2 + (ti % 2)
                i_lo = 128 * cc - 64 + qlo
                psv = pb[sb][:, :].rearrange("p (a b) -> p a b", a=2)
                for hd in range(2):
                    c.op("pe", lambda e: e.matmul(psv[0:nk, hd, qlo:qhi],
                                                  lhsT=KTv[:, r, 128 * cc:128 * cc + nk],
                                                  rhs=Qzv[hd][:, r, i_lo:i_lo + (qhi - qlo)], start=True, stop=True),
                         [RKT, RQT], [Rpb[sb]])
                es_ = ti % 2
                Ev = Ee[es_]
                c.op("act", lambda e: e.activation(out=Ev[0:nk, :, qlo:qhi], in_=psv[0:nk, :, qlo:qhi], func=AF.Exp),
                     [Rpb[sb]], [REe[es_]])
                c.op("dve", lambda e: e.tensor_tensor(out=Pp[es_][0:nk, :, qlo:qhi], in0=Ev[0:nk, :, qlo:qhi],
                                                      in1=Dt[ws][0:nk, :, qlo:qhi], op=ALU.mult),
                     [REe[es_], RDt[ws]], [RPp[es_]])

            vt_slot = {}

            def emit_vt(ti0):
                tb = proj_rr[0] % 2
                proj_rr[0] += 1
                todo = [t for t in range(ti0, min(ti0 + 4, len(tiles)))]
                pv = bview(pb[tb][:, :])
                sl0 = vslot_ctr[0] % NVS
                assert sl0 % 4 == 0
                for q_, t in enumerate(todo):
                    r, cc = tiles[t]
                    nk = 64 if cc == nq else 128
                    vt_slot[t] = sl0 + q_
                    c.op("pe", lambda e: e.transpose(pv[0:nk, q_ * 128:(q_ + 1) * 128],
                                                     VTv[:, r, 128 * cc:128 * cc + nk], ident),
                         [RVT, Rc], [Rpb[tb]])
                vslot_ctr[0] += 4
                for q_, t in enumerate(todo):
                    r, cc = tiles[t]
                    nk = 64 if cc == nq else 128
                    sl = sl0 + q_
                    base = Vt[0:nk, sl, 0, 0:64]
                    pstr = base.ap[0][0]
                    outv = bass.AP(base.tensor, base.offset, [[pstr, nk], [192, 2], [1, 64]])
                    inv = pv[0:nk, q_ * 128:(q_ + 1) * 128].rearrange("p (a b) -> p a b", a=2)
                    _ve = _os.environ.get('VT_ENG', 'alt')
                    if _ve == 'alt':
                        evac_copy(outv, inv, [Rpb[tb]], [RVt[sl]])
                    elif _ve == 'dve':
                        c.op("dve", lambda e: e.tensor_copy(out=outv, in_=inv), [Rpb[tb]], [RVt[sl]])
                    elif _ve == 'act':
                        c.op("act", lambda e: e.activation(out=outv, in_=inv, func=AF.Copy), [Rpb[tb]], [RVt[sl]])
                    elif _ve == 'none':
                        pass

            cur_bank = {}

            def emit_PV(ti):
                r, cc = tiles[ti]
                qlo, qhi = qrange(cc)
                nk = 64 if cc == nq else 128
                es_ = ti % 2
                sl = vt_slot[ti]
                for half in range(2):
                    a = max(qlo, 128 * half)
                    b = min(qhi, 128 * (half + 1))
                    if a >= b:
                        continue
                    oc = cc + half
                    xb = oc // 4
                    key = (r, xb)
                    if key not in cur_bank:
                        cur_bank[key] = ul_ctr[0] % 2
                        ul_ctr[0] += 1
                    ss = cur_bank[key]
                    col0 = (oc % 4) * 128 + (a - 128 * half)
                    first = (half == 1) or (cc == 0)
                    last = (half == 0)
                    if nq == 0:
                        first, last = True, True
                    for kind in range(2):
                        bank = 4 + 2 * ss + kind
                        for hd in range(2):
                            lhs = Vt[0:nk, sl, hd, :] if kind == 0 else (onesA, onesB)[hd][0:nk, :]
                            c.op("pe", lambda e: e.matmul(pb[bank][:, col0:col0 + (b - a)], lhsT=lhs,
                                                          rhs=Pp[es_][0:nk, hd, a:b],
                                                          start=(first and hd == 0), stop=(last and hd == 1)),
                                 [RVt[sl], RPp[es_], Rc], [Rpb[bank]])
                    done = last and ((oc % 4 == 3) or (oc == nq))
                    if done:
                        x0 = xb * 512
                        xlo = max(x0, 64)
                        xhi = min(x0 + 512, n_own + 64)
                        ilo = xlo - 64
                        cnt = xhi - xlo
                        for kind in range(2):
                            bank = 4 + 2 * ss + kind
                            acc = (Uacc, Lacc)[kind]
                            accv = acc.rearrange("p (i r) -> p r i", r=dil)[:, r, ilo:ilo + cnt]
                            src = pb[bank][:, xlo - x0:xlo - x0 + cnt]
                            if g == 0:
                                c.op("act", lambda e: e.activation(out=accv, in_=src, func=AF.Copy),
                                     [Rpb[bank]], [Racc])
                            else:
                                c.op("dve", lambda e: e.tensor_tensor(out=accv, in0=src, in1=accv, op=ALU.add),
                                     [Rpb[bank], Racc], [Racc])
                        del cur_bank[key]

            NTL = len(tiles)
            import os as _os
            _stage = int(_os.environ.get('ATT_STAGE', '9'))
            if _stage >= 2:
                emit_vt(0)
            if _stage >= 3:
                emit_S(0)
            for ti in range(NTL):
                if ti + 1 < NTL:
                    if (ti + 1) % 4 == 0 and _stage >= 2 and (ti + 1) // 4 < int(_os.environ.get('VT_MAXB', '99')):
                        emit_vt(ti + 1)
                    if _stage >= 3:
                        emit_S(ti + 1)
                if _stage >= 4:
                    emit_PV(ti)
            if g == 2:
                c.op("dve", lambda e: e.reciprocal(out=Lacc, in_=Lacc), [Racc], [Racc])
                c.op("pool", lambda e: e.tensor_tensor(out=attT[:, hp, :], in0=Uacc, in1=Lacc, op=ALU.mult),
                     [Racc], [RattT])

        if dbg == "att":
            import os as _os
            if _os.environ.get('ATT_LIMIT'):
                c.barrier()
                dump("dbg_U", Uacc, Racc)
                dump("dbg_L", Lacc, Racc)
            else:
                dump("dbg_attT", attT, RattT, BF16)
            c.barrier()
            c.finish("sp")
            return nc, dbg_outs

        c.barrier()
        A.release(markA)

        Whg = A.alloc((8, 1280), BF16)
        RWhg = Reg()
        ngb = A.alloc((2, 128), F32)
        Rng = Reg()
        c.dma("sp", ngb.rearrange("p a b -> p (a b)"), rowp_d[6:7, 0:256].partition_broadcast(128), writes=[Rng])
        vtm = A.alloc((16, 2, 128), BF16)
        Gt = A.alloc((16, 2, 128), BF16)
        gtmp = [A.alloc((256,), F32) for _ in range(2)]
        Rgtmp = [Reg() for _ in range(2)]
        Rv, RG = Reg(), Reg()
        qf = A.alloc((NOWN,), F32)
        Rq = Reg()
        qSb = A.alloc((NOWN,), BF16)
        qAb = A.alloc((NOWN,), BF16)
        kAb = A.alloc((NOWN,), BF16)
        kUb = A.alloc((NOWN,), BF16)
        Rqk = Reg()
        kUtm = A.alloc((16, 128), BF16)
        RkUtm = Reg()
        dl = A.alloc((16,), F32)
        Rdl = Reg()
        ofw = A.alloc((16, 128), F32)
        Rofw = Reg()
        osum = ofw
        Rosum = Rofw
        ssq = A.alloc((16,), F32)
        Rssq = Reg()
        junk = A.alloc((128,), F32)
        Rjunk = Reg()
        Sst = A.alloc((128,), F32)
        Ssb = [A.alloc((128,), BF16) for _ in range(2)]
        RSst = Reg()
        RSsb = [Reg() for _ in range(2)]
        ATm = [A.alloc((128,), BF16) for _ in range(2)]
        RATm = [Reg() for _ in range(2)]
        recb = [A.alloc((4, 128), BF16) for _ in range(2)]
        Rrecb = [Reg() for _ in range(2)]
        rmsc = A.alloc((1,), F32)
        c.op("pool", lambda e: e.memset(rmsc, RMS_EPS), [], [Rssq])
        htmp = [[A.alloc((512,), F32) for _ in range(4)] for _ in range(2)]
        Rhtmp = [Reg() for _ in range(2)]
        hctr = [0]

        for hp2 in range(4):
            for k in range(8):
                c.dma("pool", Whg[:, k, :], w_hg_d[hp2, :, k * 1280:(k + 1) * 1280], writes=[RWhg])
            for tt in range(16):
                bank = tt % 2
                for k in range(8):
                    c.op("pe", lambda e: e.matmul(pb[bank][:, :], lhsT=hT[:, k, tt * 128:(tt + 1) * 128],
                                                  rhs=Whg[:, k, 768:1280], start=(k == 0), stop=(k == 7)),
                         [RhT[tt // 4], RWhg], [Rpb[bank]])
                c.op("act", lambda e: e.activation(out=vtm[:, tt, :, :].rearrange("p a b -> p (a b)"),
                                                   in_=pb[bank][:, 0:256], func=AF.Copy), [Rpb[bank]], [Rv])
                gv = Gt[:, tt, :, :].rearrange("p a b -> p (a b)")
                gs = tt % 2
                c.op("act", lambda e: e.activation(out=gtmp[gs], in_=pb[bank][:, 256:512], func=AF.Sigmoid),
                     [Rpb[bank]], [Rgtmp[gs]])
                c.op("dve", lambda e: e.tensor_tensor(out=gtmp[gs], in0=pb[bank][:, 256:512], in1=gtmp[gs], op=ALU.mult),
                     [Rpb[bank], Rgtmp[gs]], [Rgtmp[gs]])
                c.op("pool", lambda e: e.tensor_tensor(out=gv, in0=gtmp[gs], in1=ngb.rearrange("p a b -> p (a b)"),
                                                       op=ALU.mult), [Rgtmp[gs], Rng], [RG])
            for hd in range(2):
                h = hp2 * 2 + hd
                for b4 in range(4):
                    bank = 2 + (b4 % 2)
                    ts_ = hctr[0] % 2
                    hctr[0] += 1
                    s1 = htmp[ts_][0]
                    for k in range(8):
                        c.op("pe", lambda e: e.matmul(pb[bank][:, :], lhsT=Whg[:, k, hd * 128:(hd + 1) * 128],
                                                      rhs=hT[:, k, b4 * 512:(b4 + 1) * 512],
                                                      start=(k == 0), stop=(k == 7)), [RhT[b4], RWhg], [Rpb[bank]])
                    c.op("act", lambda e: e.activation(out=s1, in_=pb[bank][:, :], func=AF.Sigmoid),
                         [Rpb[bank]], [Rhtmp[ts_]])
                    c.op("dve", lambda e: e.scalar_tensor_tensor(out=qf[:, b4 * 512:(b4 + 1) * 512],
                                                                 in0=pb[bank][:, :], scalar=128.0 ** -0.5, in1=s1,
                                                                 op0=ALU.mult, op1=ALU.mult),
                         [Rpb[bank], Rhtmp[ts_]], [Rq])
                c.op("pool", lambda e: e.memset(ssq, 0.0), [], [Rssq])
                for dr in range(2):
                    fcol = 256 + dr * 256 + hd * 128
                    for b4 in range(4):
                        bank = 2 + (b4 % 2)
                        ts_ = hctr[0] % 2
                        hctr[0] += 1
                        f1, k1, b1, r1 = htmp[ts_]
                        s1, t1, q1 = f1, k1, f1
                        Rt = Rhtmp[ts_]
                        bsl = slice(b4 * 512, (b4 + 1) * 512)
                        for k in range(8):
                            c.op("pe", lambda e: e.matmul(pb[bank][:, :], lhsT=Whg[:, k, fcol:fcol + 128],
                                                          rhs=hT[:, k, bsl], start=(k == 0), stop=(k == 7)),
                                 [RhT[b4], RWhg], [Rpb[bank]])
                        c.op("act", lambda e: e.activation(out=s1, in_=pb[bank][:, :], func=AF.Sigmoid),
                             [Rpb[bank]], [Rt])
                        c.op("dve", lambda e: e.tensor_scalar(out=f1, in0=s1, scalar1=omlv[:, dr, h:h + 1],
                                                              scalar2=lbv[:, dr, h:h + 1], op0=ALU.mult, op1=ALU.add),
                             [Rt, Rlb], [Rt])
                        c.op("pool", lambda e: e.tensor_scalar(out=k1, in0=f1, scalar1=-1.0, scalar2=1.0,
                                                               op0=ALU.mult, op1=ALU.add), [Rt], [Rt])
                        for j in range(4):
                            sl = slice(j * 128, (j + 1) * 128)
                            if dr == 0:
                                c.op("dve", lambda e: e.tensor_tensor_scan(out=b1[:, sl], data0=f1[:, sl],
                                                                           data1=zeros[:, 0:128], initial=1.0,
                                                                           op0=ALU.mult, op1=ALU.add), [Rt, Rz], [Rt])
                            else:
                                c.op("dve", lambda e: e.tensor_tensor_scan(out=b1[:, sl][:, ::-1],
                                                                           data0=f1[:, sl][:, ::-1],
                                                                           data1=zeros[:, 0:128], initial=1.0,
                                                                           op0=ALU.mult, op1=ALU.add), [Rt, Rz], [Rt])
                        c.op("dve", lambda e: e.reciprocal(out=r1, in_=b1), [Rt], [Rt])
                        c.op("pool", lambda e: e.tensor_tensor(out=t1, in0=k1, in1=r1, op=ALU.mult), [Rt], [Rt])
                        c.op("pool", lambda e: e.tensor_tensor(out=q1, in0=qf[:, bsl], in1=b1, op=ALU.mult),
                             [Rt, Rq], [Rt])
                        v4 = lambda ap: ap.rearrange("p (a b) -> p a b", a=4)
                        midc = 63 if dr == 0 else 64
                        lastc = 127 if dr == 0 else 0
                        bmid = v4(b1)[:, :, midc:midc + 1].to_broadcast([128, 4, 128])
                        rbmid = v4(r1)[:, :, midc:midc + 1].to_broadcast([128, 4, 128])
                        blast = v4(b1)[:, :, lastc:lastc + 1].to_broadcast([128, 4, 128])
                        c.op("act", lambda e: e.activation(out=qSb[:, bsl], in_=q1, func=AF.Copy), [Rt], [Rqk])
                        c.op("dve", lambda e: e.tensor_tensor(out=v4(qAb[:, bsl]), in0=v4(q1), in1=rbmid, op=ALU.mult),
                             [Rt], [Rqk])
                        c.op("pool", lambda e: e.tensor_tensor(out=v4(kAb[:, bsl]), in0=v4(t1), in1=bmid, op=ALU.mult),
                             [Rt], [Rqk])
                        c.op("pool", lambda e: e.tensor_tensor(out=v4(kUb[:, bsl]), in0=v4(t1), in1=blast, op=ALU.mult),
                             [Rt], [Rqk])
                        c.op("dve", lambda e: e.tensor_copy(out=dl[:, b4 * 4:(b4 + 1) * 4], in_=v4(b1)[:, :, lastc]),
                             [Rt], [Rdl])
                        tb = 4
                        for j in range(4):
                            o = bview(pb[tb][:, :])[:, j * 128:(j + 1) * 128]
                            c.op("pe", lambda e: e.transpose(o, kUb[:, b4 * 512 + j * 128:b4 * 512 + (j + 1) * 128],
                                                             ident), [Rqk, Rc], [Rpb[tb]])
                        c.op("act", lambda e: e.activation(out=kUtm[:, b4 * 4:(b4 + 1) * 4, :].rearrange("p a b -> p (a b)"),
                                                           in_=bview(pb[tb][:, :])[:, 0:512], func=AF.Copy),
                             [Rpb[tb]], [RkUtm])
                    if dr == 0:
                        c.op("pool", lambda e: e.memset(Sst, 0.0), [], [RSst])
                        c.op("pool", lambda e: e.memset(Ssb[0], 0.0), [], [RSsb[0]])
                        order = list(range(16))
                        mask = maskF
                    else:
                        c.op("pool", lambda e: e.tensor_copy(out=Sst, in_=S_in[:, h, :]), [RS], [RSst])
                        c.op("act", lambda e: e.activation(out=Ssb[0], in_=S_in[:, h, :], func=AF.Copy), [RS], [RSsb[0]])
                        order = list(range(15, -1, -1))
                        mask = maskB
                    for n_, tt in enumerate(order):
                        tsl = slice(tt * 128, (tt + 1) * 128)
                        cs = n_ % 4
                        csl = slice(cs * 128, (cs + 1) * 128)
                        sb_cur = n_ % 2
                        c.op("pe", lambda e: e.matmul(pb[5][:, csl], lhsT=kAb[:, tsl], rhs=qAb[:, tsl],
                                                      start=True, stop=True), [Rqk], [Rpb[5]])
                        am = n_ % 2
                        c.op("dve", lambda e: e.tensor_tensor(out=ATm[am], in0=pb[5][:, csl], in1=mask, op=ALU.mult),
                             [Rpb[5], Rc], [RATm[am]])
                        c.op("pe", lambda e: e.matmul(pb[6][:, csl], lhsT=kUtm[:, tt, :], rhs=vtm[:, tt, hd, :],
                                                      start=True, stop=True), [RkUtm, Rv], [Rpb[6]])
                        c.op("pe", lambda e: e.matmul(pb[7][:, csl], lhsT=ATm[am], rhs=vtm[:, tt, hd, :],
                                                      start=True, stop=False), [RATm[am], Rv], [Rpb[7]])
                        c.op("pe", lambda e: e.matmul(pb[7][:, csl], lhsT=qSb[:, tsl], rhs=Ssb[sb_cur],
                                                      start=False, stop=True), [Rqk, RSsb[sb_cur]], [Rpb[7]])
                        if n_ < 15:
                            c.op("dve", lambda e: e.scalar_tensor_tensor(out=Sst, in0=Sst, scalar=dl[:, tt:tt + 1],
                                                                         in1=pb[6][:, csl], op0=ALU.mult, op1=ALU.add),
                                 [RSst, Rdl, Rpb[6]], [RSst])
                            c.op("act", lambda e: e.activation(out=Ssb[1 - sb_cur], in_=Sst, func=AF.Copy),
                                 [RSst], [RSsb[1 - sb_cur]])
                        if dr == 0:
                            c.op("act", lambda e: e.activation(out=ofw[:, tt, :], in_=pb[7][:, csl], func=AF.Copy),
                                 [Rpb[7]], [Rofw])
                        else:
                            c.op("dve", lambda e: e.tensor_tensor(out=osum[:, tt, :], in0=pb[7][:, csl],
                                                                  in1=ofw[:, tt, :], op=ALU.add),
                                 [Rpb[7], Rofw], [Rosum])
                            c.op("act", lambda e: e.activation(out=junk, in_=osum[:, tt, :], func=AF.Square,
                                                               accum_out=ssq[:, tt:tt + 1]),
                                 [Rosum], [Rjunk, Rssq])
                c.op("act", lambda e: e.activation(out=ssq, in_=ssq, func=AF.Sqrt, bias=rmsc, scale=1.0 / 128.0),
                     [Rssq], [Rssq])
                c.op("dve", lambda e: e.reciprocal(out=ssq, in_=ssq), [Rssq], [Rssq])
                for tt in range(16):
                    rs = (tt // 4) % 2
                    c.op("dve", lambda e: e.scalar_tensor_tensor(out=recb[rs][:, tt % 4, :], in0=osum[:, tt, :],
                                                                 scalar=ssq[:, tt:tt + 1], in1=Gt[:, tt, hd, :],
                                                                 op0=ALU.mult, op1=ALU.mult),
                         [Rosum, Rssq, RG], [Rrecb[rs]])
                    if tt % 4 == 3:
                        tb = 4
                        for j in range(4):
                            o = bview(pb[tb][:, :])[:, j * 128:(j + 1) * 128]
                            c.op("pe", lambda e: e.transpose(o, recb[rs][:, j, :], ident), [Rrecb[rs], Rc], [Rpb[tb]])
                        c.op("act", lambda e: e.activation(out=recT[:, h, (tt - 3) * 128:(tt + 1) * 128],
                                                           in_=bview(pb[tb][:, :])[:, 0:512], func=AF.Copy),
                             [Rpb[tb]], [RrecT])

        if dbg == "hg":
            dump("dbg_recT", recT, RrecT, BF16)
            c.barrier()
            c.finish("sp")
            return nc, dbg_outs

        c.barrier()
        A.release(markA)

        Wa = A.alloc((4, 1024), BF16)
        Wr = A.alloc((8, 1024), BF16)
        RWa, RWr = Reg(), Reg()
        c.dma("pool", Wa.rearrange("p a b -> p (a b)"), w_a_d[:, :], writes=[RWa])
        for k in range(8):
            c.dma("pool", Wr[:, k, :], w_r_d[:, k * 1024:(k + 1) * 1024], writes=[RWr])
        Wg = [A.alloc((8, 256), BF16) for _ in range(2)]
        RWg = [Reg() for _ in range(2)]
        mT = A.alloc((8, NOWN), BF16)
        RmT = Reg()
        mt = [[A.alloc((512,), F32) for _ in range(4)] for _ in range(2)]
        Rmt = [Reg() for _ in range(2)]
        c.dma("pool", Wg[0].rearrange("p a b -> p (a b)"), w_gate_d[0, :, :], writes=[RWg[0]])
        mctr = 0
        for cc in range(8):
            ws = cc % 2
            if cc + 1 < 8:
                c.dma("pool", Wg[1 - ws].rearrange("p a b -> p (a b)"), w_gate_d[cc + 1, :, :], writes=[RWg[1 - ws]])
            for b4 in range(4):
                bsl = slice(b4 * 512, (b4 + 1) * 512)
                pbs = (mctr % 2) * 4
                ms = mctr % 2
                mctr += 1
                sa, sh, m1, m2 = mt[ms]
                for hp in range(4):
                    c.op("pe", lambda e: e.matmul(pb[pbs][:, :], lhsT=Wa[:, hp, cc * 128:(cc + 1) * 128],
                                                  rhs=attT[:, hp, bsl], start=(hp == 0), stop=(hp == 3)),
                         [RWa, RattT], [Rpb[pbs]])
                for h in range(8):
                    c.op("pe", lambda e: e.matmul(pb[pbs + 1][:, :], lhsT=Wr[:, h, cc * 128:(cc + 1) * 128],
                                                  rhs=recT[:, h, bsl], start=(h == 0), stop=(h == 7)),
                         [RWr, RrecT], [Rpb[pbs + 1]])
                for gi in range(2):
                    for k in range(8):
                        c.op("pe", lambda e: e.matmul(pb[pbs + 2 + gi][:, :], lhsT=Wg[ws][:, k, gi * 128:(gi + 1) * 128],
                                                      rhs=hT[:, k, bsl], start=(k == 0), stop=(k == 7)),
                             [RWg[ws], RhT[b4]], [Rpb[pbs + 2 + gi]])
                c.op("act", lambda e: e.activation(out=sa, in_=pb[pbs + 2][:, :], func=AF.Sigmoid),
                     [Rpb[pbs + 2]], [Rmt[ms]])
                c.op("act", lambda e: e.activation(out=sh, in_=pb[pbs + 3][:, :], func=AF.Sigmoid),
                     [Rpb[pbs + 3]], [Rmt[ms]])
                c.op("dve", lambda e: e.tensor_tensor(out=m1, in0=pb[pbs][:, :], in1=sa, op=ALU.mult),
                     [Rpb[pbs], Rmt[ms]], [Rmt[ms]])
                c.op("dve", lambda e: e.tensor_tensor(out=m2, in0=pb[pbs + 1][:, :], in1=sh, op=ALU.mult),
                     [Rpb[pbs + 1], Rmt[ms]], [Rmt[ms]])
                c.op("pool", lambda e: e.tensor_tensor(out=mT[:, cc, bsl], in0=m1, in1=m2, op=ALU.add),
                     [Rmt[ms]], [RmT])

        if dbg == "merge":
            dump("dbg_mT", mT, RmT, BF16)
            c.barrier()
            c.finish("sp")
            return nc, dbg_outs

        c.barrier()
        A.release(base_mark)
        mT2 = A.alloc((8, NOWN), BF16)
        RmT2 = Reg()
        for k in range(8):
            eng = ("dve", "pool")[k % 2]
            c.op(eng, lambda e: e.tensor_copy(out=mT2[:, k, :], in_=mT[:, k, :]), [RmT], [RmT2])
        c.barrier()

        Wo = A.alloc((8, 1024), BF16)
        RWo = Reg()
        for k in range(8):
            c.dma("pool", Wo[:, k, :], w_o_d[:, k * 1024:(k + 1) * 1024], writes=[RWo])
        rows = A.alloc((6, D), F32)
        Rrows = Reg()
        for i in range(6):
            c.dma("sp", rows[:, i, :], rowp_d[i:i + 1, :].partition_broadcast(128), writes=[Rrows])
        for i in range(2):
            c.op("pool", lambda e: e.tensor_scalar(out=rows[:, i, :], in0=rows[:, i, :], scalar1=ALPHA, scalar2=None,
                                                   op0=ALU.mult), [Rrows], [Rrows])
        NQT = 4
        h1 = A.alloc((NQT, D), F32)
        Rh1 = [Reg() for _ in range(NQT)]
        h1T = A.alloc((8, 512), BF16)
        Rh1T = Reg()
        gT = A.alloc((NJ, 512), BF16)
        RgT = Reg()
        xr = [A.alloc((D,), F32) for _ in range(2)]
        Rxr = [Reg() for _ in range(2)]
        zt = [A.alloc((D,), F32) for _ in range(2)]
        Rzt = [Reg() for _ in range(2)]
        hb16 = [A.alloc((D,), BF16) for _ in range(2)]
        Rhb16 = [Reg() for _ in range(2)]
        st2 = A.alloc((2, 2, 6), F32)
        mv2 = A.alloc((2, 2), F32)
        rs2 = A.alloc((2, 2), F32)
        Rst2 = [Reg() for _ in range(2)]
        W1 = [A.alloc((8, 256), BF16) for _ in range(3)]
        RW1 = [Reg() for _ in range(3)]
        W2 = [A.alloc((NJ, 128), BF16) for _ in range(2)]
        RW2 = [Reg() for _ in range(2)]
        sgt = [A.alloc((512,), F32) for _ in range(2)]
        Rsgt = [Reg() for _ in range(2)]
        fT = [A.alloc((512,), F32) for _ in range(2)]
        RfT = [Reg() for _ in range(2)]
        yo = [A.alloc((D,), F32) for _ in range(2)]
        Ryo = [Reg() for _ in range(2)]
        epsc2 = A.alloc((1,), F32)
        c.op("pool", lambda e: e.memset(epsc2, LN_EPS), [], [Rst2[0], Rst2[1]])

        def ln_tile(zs, out_ap, gi, Rout):
            z = zt[zs]
            slot = zs
            Rs = Rst2[slot]
            for hh in range(2):
                c.op("dve", lambda e: e.bn_stats(out=st2[:, slot, hh, :], in_=z[:, hh * 512:(hh + 1) * 512]),
                     [Rzt[zs]], [Rs])
            c.op("dve", lambda e: e.bn_aggr(out=mv2[:, slot, :], in_=st2[:, slot, :, :].rearrange("p a b -> p (a b)")),
                 [Rs], [Rs])
            c.op("act", lambda e: e.activation(out=rs2[:, slot, 0:1], in_=mv2[:, slot, 1:2], func=AF.Sqrt,
                                               bias=epsc2, scale=1.0), [Rs], [Rs])
            c.op("dve", lambda e: e.reciprocal(out=rs2[:, slot, 0:1], in_=rs2[:, slot, 0:1]), [Rs], [Rs])
            c.op("dve", lambda e: e.scalar_tensor_tensor(out=rs2[:, slot, 1:2], in0=mv2[:, slot, 0:1], scalar=-1.0,
                                                         in1=rs2[:, slot, 0:1], op0=ALU.mult, op1=ALU.mult), [Rs], [Rs])
            c.op("act", lambda e: e.activation(out=z, in_=z, func=AF.Identity, bias=rs2[:, slot, 1:2],
                                               scale=rs2[:, slot, 0:1]), [Rzt[zs], Rs], [Rzt[zs]])
            c.op("pool", lambda e: e.tensor_tensor(out=z, in0=z, in1=rows[:, gi, :], op=ALU.mult),
                 [Rzt[zs], Rrows], [Rzt[zs]])
            c.op("pool", lambda e: e.tensor_tensor(out=out_ap, in0=z, in1=rows[:, gi + 1, :], op=ALU.add),
                 [Rzt[zs], Rrows], [Rout])

        def load_w1(j, s):
            c.dma("pool", W1[s].rearrange("p a b -> p (a b)"), w_f1_d[j, :, :], writes=[RW1[s]])

        def load_w2(cc, s):
            c.dma("pool", W2[s].rearrange("p a b -> p (a b)"), w_f2_d[cc, :, :], writes=[RW2[s]])

        tctr = 0
        fctr = 0
        for qtr in range(4):
            for tl in range(NQT):
                tg = qtr * NQT + tl
                tsl = slice(tg * 128, (tg + 1) * 128)
                s = tctr % 2
                tctr += 1
                c.dma("sp", xr[s], x_d[tg * 128:(tg + 1) * 128, :], writes=[Rxr[s]])
                for hh in range(2):
                    bank = 2 * s + hh
                    for k in range(8):
                        c.op("pe", lambda e: e.matmul(pb[bank][:, :], lhsT=mT2[:, k, tsl],
                                                      rhs=Wo[:, k, hh * 512:(hh + 1) * 512],
                                                      start=(k == 0), stop=(k == 7)), [RmT2, RWo], [Rpb[bank]])
                c.op("act", lambda e: e.activation(out=xr[s], in_=xr[s], func=AF.Identity,
                                                   bias=mstat[:, tg, 1:2], scale=mstat[:, tg, 0:1]),
                     [Rxr[s], Rms], [Rxr[s]])
                c.op("pool", lambda e: e.tensor_tensor(out=xr[s], in0=xr[s], in1=rows[:, 0, :], op=ALU.mult),
                     [Rxr[s], Rrows], [Rxr[s]])
                c.op("pool", lambda e: e.tensor_tensor(out=xr[s], in0=xr[s], in1=rows[:, 1, :], op=ALU.add),
                     [Rxr[s], Rrows], [Rxr[s]])
                for hh in range(2):
                    bank = 2 * s + hh
                    c.op("dve", lambda e: e.tensor_tensor(out=zt[s][:, hh * 512:(hh + 1) * 512], in0=pb[bank][:, :],
                                                          in1=xr[s][:, hh * 512:(hh + 1) * 512], op=ALU.add),
                         [Rpb[bank], Rxr[s]], [Rzt[s]])
                ln_tile(s, h1[:, tl, :], 2, Rh1[tl])
                c.op("act", lambda e: e.activation(out=hb16[s], in_=h1[:, tl, :], func=AF.Copy), [Rh1[tl]], [Rhb16[s]])
                tb = 4 + s
                for k in range(8):
                    o = bview(pb[tb][:, :])[:, k * 128:(k + 1) * 128]
                    c.op("pe", lambda e: e.transpose(o, hb16[s][:, k * 128:(k + 1) * 128], ident),
                         [Rhb16[s], Rc], [Rpb[tb]])
                evac_copy(h1T[:, :, tl * 128:(tl + 1) * 128],
                          bview(pb[tb][:, :]).rearrange("p (a b) -> p a b", a=8), [Rpb[tb]], [Rh1T])
            load_w1(0, 0)
            load_w1(1, 1)
            for j in range(NJ):
                ws = j % 3
                if j + 2 < NJ:
                    load_w1(j + 2, (j + 2) % 3)
                fs = fctr % 2
                fctr += 1
                pbg = 2 * fs
                for gi in range(2):
                    for k in range(8):
                        c.op("pe", lambda e: e.matmul(pb[pbg + gi][:, :], lhsT=W1[ws][:, k, gi * 128:(gi + 1) * 128],
                                                      rhs=h1T[:, k, :], start=(k == 0), stop=(k == 7)),
                             [RW1[ws], Rh1T], [Rpb[pbg + gi]])
                c.op("act", lambda e: e.activation(out=sgt[fs], in_=pb[pbg][:, :], func=AF.Silu), [Rpb[pbg]], [Rsgt[fs]])
                c.op("dve", lambda e: e.tensor_tensor(out=gT[:, j, :], in0=pb[pbg + 1][:, :], in1=sgt[fs], op=ALU.mult),
                     [Rpb[pbg + 1], Rsgt[fs]], [RgT])
            load_w2(0, 0)
            for cc in range(8):
                ws = cc % 2
                if cc + 1 < 8:
                    load_w2(cc + 1, 1 - ws)
                fs = cc % 2
                bank = 4 + fs
                for j in range(NJ):
                    c.op("pe", lambda e: e.matmul(pb[bank][:, :], lhsT=W2[ws][:, j, :], rhs=gT[:, j, :],
                                                  start=(j == 0), stop=(j == NJ - 1)), [RW2[ws], RgT], [Rpb[bank]])
                evac_copy(fT[fs], pb[bank][:, :], [Rpb[bank]], [RfT[fs]])
                tb = 6 + fs
                for q_ in range(4):
                    c.op("pe", lambda e: e.transpose(pb[tb][:, q_ * 128:(q_ + 1) * 128],
                                                     fT[fs][:, q_ * 128:(q_ + 1) * 128], identf),
                         [RfT[fs], Rc], [Rpb[tb]])
                for q_ in range(4):
                    hv = h1[:, q_, cc * 128:(cc + 1) * 128]
                    c.op("dve", lambda e: e.scalar_tensor_tensor(out=hv, in0=hv, scalar=ALPHA,
                                                                 in1=pb[tb][:, q_ * 128:(q_ + 1) * 128],
                                                                 op0=ALU.mult, op1=ALU.add),
                         [Rh1[q_], Rpb[tb]], [Rh1[q_]])
            for tl in range(NQT):
                tg = qtr * NQT + tl
                s = tctr % 2
                tctr += 1
                c.op("pool", lambda e: e.tensor_copy(out=zt[s], in_=h1[:, tl, :]), [Rh1[tl]], [Rzt[s]])
                ln_tile(s, yo[s], 4, Ryo[s])
                c.dma("sp", y_d[tg * 128:(tg + 1) * 128, :], yo[s], reads=[Ryo[s]])
        c.barrier()
        c.finish("sp")
    return nc, dbg_outs


IN_SPLITS = (1536, 1536, 1536, 1024, 1024, 1024, 1024, 1024, 1024, 1024)
OFF = np.concatenate([[0], np.cumsum(IN_SPLITS)]).astype(int)


def _kmajor(w):
    n = w.shape[1]
    return np.ascontiguousarray(w.reshape(8, 128, n).transpose(1, 0, 2).reshape(128, 8 * n))


def _consts():
    cst = np.zeros((128, 640), np.float32)
    cst[:, 0:128] = np.eye(128, dtype=np.float32)
    j = np.arange(128)[:, None]
    i = np.arange(128)[None, :]
    cst[:, 128:256] = (j <= i)
    cst[:, 256:384] = (j >= i)
    cst[:, 384:448] = 1.0
    cst[:, 576:640] = 1.0
    slopes = (2.0 ** (-8.0 * np.arange(1, 25, dtype=np.float32) / 24.0)).astype(np.float32).reshape(3, 8)
    kk = np.arange(128)[:, None]
    qq = np.arange(256)[None, :]
    rel = kk - qq + 64
    valid = np.abs(rel) <= 64
    dtab = np.zeros((12, 128, 512), np.float32)
    for g, (win, dil) in enumerate(GROUPS):
        for hp in range(4):
            for hd in range(2):
                sl = slopes[g, hp * 2 + hd]
                bias = (-sl * (dil * np.abs(rel)).astype(np.float32)).astype(np.float32)
                dtab[g * 4 + hp, :, hd * 256:(hd + 1) * 256] = np.where(valid, np.exp(bias), 0.0)
    return cst, dtab


def _prep_weights(side, w_in, hgrn_lb, ln_in_g, ln_in_b):
    w = w_in[0]
    seg = lambda i: w[:, OFF[i]:OFF[i + 1]]
    aq, ak, av, hq, hf0, hf1, hi, hg, ga, gh = [seg(i) for i in range(10)]
    hfF, hfB = (hf0, hf1) if side == 0 else (hf1, hf0)
    w_att = np.empty((12, 128, 8 * 384), np.float32)
    for hp in range(4):
        for g in range(3):
            cs = slice(g * 512 + hp * 128, g * 512 + (hp + 1) * 128)
            grp = np.concatenate([aq[:, cs], ak[:, cs], av[:, cs]], axis=1)
            w_att[hp * 3 + g] = _kmajor(grp)
    w_hg = np.empty((4, 128, 8 * 1280), np.float32)
    for hp in range(4):
        cs = slice(hp * 256, (hp + 1) * 256)
        grp = np.concatenate([hq[:, cs], hfF[:, cs], hfB[:, cs], hi[:, cs], hg[:, cs]], axis=1)
        w_hg[hp] = _kmajor(grp)
    w_gate = np.empty((8, 128, 8 * 256), np.float32)
    for cc in range(8):
        cs = slice(cc * 128, (cc + 1) * 128)
        w_gate[cc] = _kmajor(np.concatenate([ga[:, cs], gh[:, cs]], axis=1))
    w_pre = _kmajor(np.concatenate([hfB, hi], axis=1))
    pp = np.zeros((128, 48), np.float32)
    pp[:, 0:8] = ln_in_g.reshape(8, 128).T
    pp[:, 8:16] = ln_in_b.reshape(8, 128).T
    dF, dB = (0, 1) if side == 0 else (1, 0)
    pp[:, 16:24] = hgrn_lb[dF, 0].reshape(8, 128).T
    pp[:, 24:32] = hgrn_lb[dF, 1].reshape(8, 128).T
    pp[:, 32:40] = hgrn_lb[dB, 0].reshape(8, 128).T
    pp[:, 40:48] = hgrn_lb[dB, 1].reshape(8, 128).T
    return dict(w_att=w_att, w_hg=w_hg, w_gate=w_gate, w_pre=w_pre, pp=pp)


def make_in_maps(x, ln_in_g, ln_in_b, w_in, hgrn_lb, hgrn_norm_g, w_att_up, w_hgrn_up, w_o,
                 ln1_g, ln1_b, w_ffn_in, w_ffn_out, ln2_g, ln2_b):
    f = lambda a: np.asarray(a, dtype=np.float32)
    x, ln_in_g, ln_in_b, w_in, hgrn_lb = f(x), f(ln_in_g), f(ln_in_b), f(w_in), f(hgrn_lb)
    cst, dtab = _consts()
    rowp = np.stack([ln_in_g, ln_in_b, f(ln1_g)[0], f(ln1_b)[0], f(ln2_g)[0], f(ln2_b)[0],
                     np.tile(f(hgrn_norm_g)[0], 8)], axis=0).astype(np.float32)
    wa = f(w_att_up)[0]
    w_a = np.ascontiguousarray(wa.reshape(4, 128, 1024).transpose(1, 0, 2).reshape(128, 4096))
    w_r = _kmajor(f(w_hgrn_up)[0])
    w_o_ = _kmajor(f(w_o)[0])
    w1 = f(w_ffn_in)[0]
    w_f1 = np.empty((NJ, 128, 8 * 256), np.float32)
    for j in range(NJ):
        grp = np.concatenate([w1[:, j * 128:(j + 1) * 128], w1[:, DFF + j * 128:DFF + (j + 1) * 128]], axis=1)
        w_f1[j] = _kmajor(grp)
    w2 = f(w_ffn_out)[0]
    w_f2 = np.empty((8, 128, NJ * 128), np.float32)
    for cc in range(8):
        w_f2[cc] = w2[:, cc * 128:(cc + 1) * 128].reshape(NJ, 128, 128).transpose(1, 0, 2).reshape(128, NJ * 128)
    sides = [_prep_weights(s, w_in, hgrn_lb, ln_in_g, ln_in_b) for s in range(2)]
    common = dict(cst=cst, rowp=rowp, dtab=dtab, w_a=w_a, w_r=w_r, w_o=w_o_, w_f1=w_f1, w_f2=w_f2)
    in_maps = []
    for core in range(8):
        b, side = core // 2, core % 2
        xc = x[b] if side == 0 else x[b][::-1]
        m = dict(common)
        m.update(sides[side])
        m["x"] = np.ascontiguousarray(xc)
        in_maps.append(m)
    return in_maps


_NC_CACHE = {}


def kernel(**inputs):
    in_maps = make_in_maps(**inputs)
    if "nc" not in _NC_CACHE:
        _NC_CACHE["nc"] = build_nc()[0]
    nc = _NC_CACHE["nc"]
    res = run_bass_kernel_spmd(nc, in_maps, core_ids=list(range(8)))
    out = np.empty((4, S, D), np.float32)
    for core in range(8):
        b, side = core // 2, core % 2
        y = np.asarray(res.results[core]["y"], dtype=np.float32)
        if side == 0:
            out[b, 0:NOWN] = y
        else:
            out[b, NOWN:S] = y[::-1]
    return out
```
